# Optimizing a Trainium2 kernel written in Bass

```python
import math
import jax, jax.numpy as jnp
from jax import lax
import numpy as np

D_MODEL = 2048
BATCH = 4
SEQ = 4096
DEPTH = 2

N_MEM = 256
MLSTM_WIDTH = D_MODEL // 2
MLSTM_HEADS = 4
MLSTM_HEAD_DIM = MLSTM_WIDTH // MLSTM_HEADS
MLSTM_CHUNK = 64
CONV_WIDTH = 4
ATTN_WIDTH = D_MODEL - MLSTM_WIDTH
ATTN_HEAD_DIM = 128
ATTN_HEADS = ATTN_WIDTH // ATTN_HEAD_DIM
DILATED_PATTERNS = ((128, 1), (512, 4), (2048, 16))
ATTN_BLOCK = 128
REL_BUCKETS = 32
REL_MAX_DISTANCE = 2048
XATTN_HEADS = 4
XATTN_HEAD_DIM = D_MODEL // XATTN_HEADS
FFN_HIDDEN = -(-8 * D_MODEL // (3 * 256)) * 256
DEEPNORM_ALPHA = (2 * DEPTH) ** 0.25
DEEPNORM_BETA = (8 * DEPTH) ** -0.25
LN_EPS = 1e-5
IN_SPLITS = (MLSTM_WIDTH, MLSTM_WIDTH, MLSTM_WIDTH, MLSTM_WIDTH, MLSTM_HEADS, MLSTM_HEADS,
             ATTN_WIDTH, ATTN_WIDTH, ATTN_WIDTH)
IN_WIDTH = sum(IN_SPLITS)

kernel_name = "hymba_mlstm_dilated_attn_deepnorm"


def split_columns(a, sizes):
    bounds = np.cumsum(sizes)[:-1].tolist()
    return jnp.split(a, bounds, axis=-1)


def layer_norm(x, g, b):
    xf = x.astype(jnp.float32)
    mu = xf.mean(-1, keepdims=True)
    var = jnp.square(xf - mu).mean(-1, keepdims=True)
    return ((xf - mu) * lax.rsqrt(var + LN_EPS) * g + b).astype(x.dtype)


def causal_depthwise_conv(x, w, b):
    c = x.shape[-1]
    y = lax.conv_general_dilated(x, w[:, None, :].astype(x.dtype), window_strides=(1,),
                                 padding=[(CONV_WIDTH - 1, 0)],
                                 dimension_numbers=('NWC', 'WIO', 'NWC'),
                                 feature_group_count=c)
    return y + b


def mlstm_chunk_step(carry, xs):
    c_state, n_state, m_state = carry
    q, k, v, log_i, log_f = xs
    L = q.shape[2]
    causal = jnp.tril(jnp.ones((L, L), dtype=bool))
    b = jnp.cumsum(log_f, axis=-1)
    d_log = jnp.where(causal, b[..., :, None] - b[..., None, :] + log_i[..., None, :], -jnp.inf)
    inter = b + m_state[..., None]
    m_row = jnp.maximum(inter, d_log.max(-1))
    s_w = jnp.exp(d_log - m_row[..., None]) * jnp.einsum('bhsd,bhrd->bhsr', q, k)
    w_inter = jnp.exp(inter - m_row)
    num = jnp.einsum('bhsr,bhre->bhse', s_w, v) + w_inter[..., None] * jnp.einsum('bhsd,bhde->bhse', q, c_state)
    den = s_w.sum(-1) + w_inter * jnp.einsum('bhsd,bhd->bhs', q, n_state)
    h = num / jnp.maximum(jnp.abs(den), jnp.exp(-m_row))[..., None]
    b_last = b[..., -1]
    g = b_last[..., None] - b + log_i
    m_new = jnp.maximum(b_last + m_state, g.max(-1))
    wk = jnp.exp(g - m_new[..., None])
    decay = jnp.exp(b_last + m_state - m_new)
    c_new = decay[..., None, None] * c_state + jnp.einsum('bhr,bhrd,bhre->bhde', wk, k, v)
    n_new = decay[..., None] * n_state + jnp.einsum('bhr,bhrd->bhd', wk, k)
    return (c_new, n_new, m_new), h


def mlstm_mixer(q_pre, k_pre, v_pre, o_pre, i_pre, f_pre, conv_w, conv_b, gate_b, mh_g):
    B, S, _ = q_pre.shape
    H, dh = MLSTM_HEADS, MLSTM_HEAD_DIM
    qk = jax.nn.silu(causal_depthwise_conv(jnp.concatenate([q_pre, k_pre], -1), conv_w, conv_b))
    q, k = jnp.split(qk, 2, axis=-1)
    heads = lambda a: a.reshape(B, S, H, dh).transpose(0, 2, 1, 3).astype(jnp.float32)
    q, k, v = heads(q), heads(k) * (dh ** -0.5), heads(v_pre)
    gates = (jnp.concatenate([i_pre, f_pre], -1) + gate_b).astype(jnp.float32)
    log_i = gates[..., :H].transpose(0, 2, 1)
    log_f = jax.nn.log_sigmoid(gates[..., H:]).transpose(0, 2, 1)
    nc = S // MLSTM_CHUNK
    chunks = lambda a: jnp.moveaxis(a.reshape(a.shape[:2] + (nc, MLSTM_CHUNK) + a.shape[3:]), 2, 0)
    init = (jnp.zeros((B, H, dh, dh), jnp.float32), jnp.zeros((B, H, dh), jnp.float32),
            jnp.zeros((B, H), jnp.float32))
    _, h = lax.scan(mlstm_chunk_step, init,
                    (chunks(q), chunks(k), chunks(v), chunks(log_i), chunks(log_f)))
    h = jnp.moveaxis(h, 0, 2).reshape(B, H, S, dh)
    mu = h.mean(-1, keepdims=True)
    var = jnp.square(h - mu).mean(-1, keepdims=True)
    hn = ((h - mu) * lax.rsqrt(var + LN_EPS)).transpose(0, 2, 1, 3).reshape(B, S, MLSTM_WIDTH) * mh_g
    return (jax.nn.sigmoid(o_pre.astype(jnp.float32)) * hn).astype(q_pre.dtype)


def t5_bucket(dist):
    exact = REL_BUCKETS // 2
    d = jnp.maximum(dist, 1).astype(jnp.float32)
    log_bucket = exact + (jnp.log(d / exact) / math.log(REL_MAX_DISTANCE / exact)
                          * (REL_BUCKETS - exact)).astype(jnp.int32)
    return jnp.where(dist < exact, dist, jnp.minimum(log_bucket, REL_BUCKETS - 1))


def dilated_pattern(q, k, v, rel_bias, window, dil):
    B, H, S, dh = q.shape
    L = S // dil
    nb = -(-L // ATTN_BLOCK)
    pad = nb * ATTN_BLOCK - L
    reach = window // dil

    def to_blocks(a):
        a = a.reshape(B, H, L, dil, dh).transpose(0, 1, 3, 2, 4)
        a = jnp.pad(a, ((0, 0), (0, 0), (0, 0), (0, pad), (0, 0)))
        return a.reshape(B, H, dil, nb, ATTN_BLOCK, dh)

    def with_prev(a):
        prev = jnp.concatenate([jnp.zeros_like(a[:, :, :, :1]), a[:, :, :, :-1]], axis=3)
        return jnp.concatenate([prev, a], axis=4)

    qb = to_blocks(q)
    kc, vc = with_prev(to_blocks(k)), with_prev(to_blocks(v))
    qi = jnp.arange(ATTN_BLOCK)[:, None]
    kj = jnp.arange(2 * ATTN_BLOCK)[None, :]
    delta = qi + ATTN_BLOCK - kj
    blk = jnp.arange(nb)[:, None, None]
    valid = (delta >= 0) & (delta <= reach) & (blk * ATTN_BLOCK + kj - ATTN_BLOCK >= 0)
    bias = rel_bias[t5_bucket(jnp.maximum(delta, 0) * dil)].transpose(2, 0, 1).astype(jnp.float32)
    s = jnp.einsum('bhrnqd,bhrnkd->bhrnqk', qb, kc) * (dh ** -0.5) + bias[:, None, None]
    s = jnp.where(valid, s, -jnp.inf)
    lse = jax.nn.logsumexp(s, axis=-1)
    p = jnp.exp(s - lse[..., None])
    o = jnp.einsum('bhrnqk,bhrnkd->bhrnqd', p, vc)
    o = o.reshape(B, H, dil, nb * ATTN_BLOCK, dh)[:, :, :, :L].transpose(0, 1, 3, 2, 4).reshape(B, H, S, dh)
    lse = lse.reshape(B, H, dil, nb * ATTN_BLOCK)[:, :, :, :L].transpose(0, 1, 3, 2).reshape(B, H, S)
    return o, lse


def dilated_attention(q_pre, k_pre, v_pre, rel_bias):
    B, S, _ = q_pre.shape
    heads = lambda a: a.reshape(B, S, ATTN_HEADS, ATTN_HEAD_DIM).transpose(0, 2, 1, 3).astype(jnp.float32)
    q, k, v = heads(q_pre), heads(k_pre), heads(v_pre)
    outs, lses = [], []
    for window, dil in DILATED_PATTERNS:
        o, lse = dilated_pattern(q, k, v, rel_bias, window, dil)
        outs.append(o)
        lses.append(lse)
    w = jax.nn.softmax(jnp.stack(lses, 0), axis=0)
    o = jnp.einsum('pbhs,pbhsd->bhsd', w, jnp.stack(outs, 0))
    return o.transpose(0, 2, 1, 3).reshape(B, S, ATTN_WIDTH).astype(q_pre.dtype)


def memory_cross_attention(x, mem, wq, wk, wv, wo):
    B, S, _ = x.shape
    M = mem.shape[1]
    q = (x @ wq).reshape(B, S, XATTN_HEADS, XATTN_HEAD_DIM)
    k = (mem @ wk).reshape(B, M, XATTN_HEADS, XATTN_HEAD_DIM)
    v = (mem @ wv).reshape(B, M, XATTN_HEADS, XATTN_HEAD_DIM)
    s = jnp.einsum('bshd,bmhd->bhsm', q, k).astype(jnp.float32) * (XATTN_HEAD_DIM ** -0.5)
    p = jax.nn.softmax(s, axis=-1)
    o = jnp.einsum('bhsm,bmhd->bshd', p, v.astype(jnp.float32)).reshape(B, S, D_MODEL).astype(x.dtype)
    return o @ wo


def swiglu_ffn(x, w1, w3, w2):
    return (jax.nn.silu(x @ w1) * (x @ w3)) @ w2


def setup_inputs(seed: int = 0) -> dict:
    key = jax.random.key(seed)
    ks = jax.random.split(key, 24)
    nrm = lambda k, shape, scale: jax.random.normal(k, shape, jnp.float32) * scale
    beta = DEEPNORM_BETA
    col_scales = (1.0, 1.0, beta, 1.0, 1.0, 1.0, 1.0, 1.0, beta)
    col_scale = jnp.concatenate([jnp.full((n,), s, jnp.float32) for n, s in zip(IN_SPLITS, col_scales)])
    x = nrm(ks[0], (BATCH, SEQ, D_MODEL), 1.0)
    mem = nrm(ks[1], (BATCH, N_MEM, D_MODEL), 1.0)
    w_in = nrm(ks[2], (DEPTH, D_MODEL, IN_WIDTH), D_MODEL ** -0.5) * col_scale
    forget_init = jnp.tile(jnp.linspace(3.0, 6.0, MLSTM_HEADS, dtype=jnp.float32)[None], (DEPTH, 1))
    gate_b = jnp.concatenate([nrm(ks[3], (DEPTH, MLSTM_HEADS), 0.1),
                              forget_init + nrm(ks[4], (DEPTH, MLSTM_HEADS), 0.1)], axis=-1)
    conv_w = nrm(ks[5], (DEPTH, CONV_WIDTH, 2 * MLSTM_WIDTH), CONV_WIDTH ** -0.5)
    conv_b = nrm(ks[6], (DEPTH, 2 * MLSTM_WIDTH), 0.02)
    mh_g = 1.0 + nrm(ks[7], (DEPTH, MLSTM_WIDTH), 0.02)
    rel_bias = nrm(ks[8], (REL_BUCKETS, ATTN_HEADS), 0.5)
    w_out = nrm(ks[9], (DEPTH, D_MODEL, D_MODEL), D_MODEL ** -0.5) * beta
    xq = nrm(ks[10], (DEPTH, D_MODEL, D_MODEL), D_MODEL ** -0.5)
    xk = nrm(ks[11], (DEPTH, D_MODEL, D_MODEL), D_MODEL ** -0.5)
    xv = nrm(ks[12], (DEPTH, D_MODEL, D_MODEL), D_MODEL ** -0.5) * beta
    xo = nrm(ks[13], (DEPTH, D_MODEL, D_MODEL), D_MODEL ** -0.5) * beta
    w1 = nrm(ks[14], (DEPTH, D_MODEL, FFN_HIDDEN), D_MODEL ** -0.5)
    w3 = nrm(ks[15], (DEPTH, D_MODEL, FFN_HIDDEN), D_MODEL ** -0.5)
    w2 = nrm(ks[16], (DEPTH, FFN_HIDDEN, D_MODEL), FFN_HIDDEN ** -0.5) * beta
    ln_g = 1.0 + nrm(ks[17], (DEPTH, 3, D_MODEL), 0.02)
    ln_b = nrm(ks[18], (DEPTH, 3, D_MODEL), 0.02)
    return {"x": x, "mem": mem, "w_in": w_in, "gate_b": gate_b, "conv_w": conv_w, "conv_b": conv_b,
            "mh_g": mh_g, "rel_bias": rel_bias, "w_out": w_out, "xq": xq, "xk": xk, "xv": xv,
            "xo": xo, "w1": w1, "w3": w3, "w2": w2, "ln_g": ln_g, "ln_b": ln_b}


def reference(x, mem, w_in, gate_b, conv_w, conv_b, mh_g, rel_bias, w_out, xq, xk, xv, xo,
              w1, w3, w2, ln_g, ln_b):
    for l in range(DEPTH):
        proj = x @ w_in[l]
        q_m, k_m, v_m, o_m, i_m, f_m, q_a, k_a, v_a = split_columns(proj, IN_SPLITS)
        y_m = mlstm_mixer(q_m, k_m, v_m, o_m, i_m, f_m, conv_w[l], conv_b[l], gate_b[l], mh_g[l])
        y_a = dilated_attention(q_a, k_a, v_a, rel_bias)
        y = jnp.concatenate([y_m, y_a], axis=-1) @ w_out[l]
        x = layer_norm(DEEPNORM_ALPHA * x + y, ln_g[l, 0], ln_b[l, 0])
        y = memory_cross_attention(x, mem, xq[l], xk[l], xv[l], xo[l])
        x = layer_norm(DEEPNORM_ALPHA * x + y, ln_g[l, 1], ln_b[l, 1])
        y = swiglu_ffn(x, w1[l], w3[l], w2[l])
        x = layer_norm(DEEPNORM_ALPHA * x + y, ln_g[l, 2], ln_b[l, 2])
    return x
```

```python
import contextlib
import math
import numpy as np
import ml_dtypes
import concourse.bass as bass
import concourse.mybir as mybir
from concourse.bass_utils import run_bass_kernel_spmd

F32 = mybir.dt.float32
BF16 = mybir.dt.bfloat16
AF = mybir.ActivationFunctionType
ALU = mybir.AluOpType
NPBF = ml_dtypes.bfloat16

D = 2048
SEQ = 4096
NB = 4
DEPTH = 2
NMEM = 256
FFN = 5632
INW = 7176
TT = 512
ALPHA = (2 * DEPTH) ** 0.25
EPS = 1e-5

import os
OPT = os.environ.get("KOPT", "abcd")
ENGS = ("pe", "act", "dve", "pool", "sp")
N_DMA_SEMS = 6


class Op:
    __slots__ = ("eng", "fn", "waits", "signal", "dma", "dsem", "dval", "idx", "cnt")

    def __init__(self, eng, fn, dma):
        self.eng = eng
        self.fn = fn
        self.waits = {}
        self.signal = False
        self.dma = dma
        self.dsem = None
        self.dval = 0
        self.idx = -1
        self.cnt = 0


class Sched:
    def __init__(self, nc, es):
        self.nc = nc
        self.ops = {e: [] for e in ENGS}
        self.state = {}
        self.dma_last = {}
        self.dma_rr = {e: 0 for e in ENGS}
        self.dma_cnt = {}
        self.esem = {e: es.enter_context(nc.semaphore("s_" + e)) for e in ENGS}
        self.dsem = {}
        for e in ("sp", "pool"):
            for i in range(N_DMA_SEMS):
                self.dsem[(e, i)] = es.enter_context(nc.semaphore("d_%s_%d" % (e, i)))
        self.dsem[("pool", "cc")] = es.enter_context(nc.semaphore("d_cc"))
        self.seen = {e: {} for e in ENGS}
        self.sigcnt = {e: 0 for e in ENGS}
        self.emitted = {e: 0 for e in ENGS}

    def _dep(self, op, src):
        if src is None or src is op:
            return
        if src.dma:
            key = ("d", src.eng, src.dsem)
            op.waits[key] = max(op.waits.get(key, 0), src.dval)
        else:
            if src.eng == "pe" and op.eng == "pe" and not op.dma:
                return
            key = ("e", src.eng)
            src.signal = True
            lst = op.waits.get(key)
            if lst is None or lst.idx < src.idx:
                op.waits[key] = src

    def add(self, eng, fn, r=(), w=(), dma=False, cc=False):
        op = Op(eng, fn, dma or cc)
        op.idx = len(self.ops[eng])
        if dma or cc:
            if cc:
                s = "cc"
            else:
                s = self.dma_rr[eng] % N_DMA_SEMS
                self.dma_rr[eng] += 1
            op.dsem = s
            k = (eng, s)
            prev = self.dma_last.get(k)
            if prev is not None and not cc:
                key = ("d", eng, s)
                op.waits[key] = max(op.waits.get(key, 0), prev.dval)
            op.dval = self.dma_cnt.get(k, 0) + (1 if cc else 16)
            self.dma_cnt[k] = op.dval
            self.dma_last[k] = op
        for t in r:
            st = self.state.setdefault(t, [None, []])
            self._dep(op, st[0])
        for t in w:
            st = self.state.setdefault(t, [None, []])
            self._dep(op, st[0])
            for rd in st[1]:
                self._dep(op, rd)
        for t in r:
            self.state[t][1].append(op)
        for t in w:
            self.state[t] = [op, []]
        self.ops[eng].append(op)
        return op

    def barrier(self):
        lasts = []
        for e in ENGS:
            for op in reversed(self.ops[e]):
                if op.fn is not None and not op.dma:
                    lasts.append(op)
                    break
        dl = list(self.dma_last.values())
        for e in ENGS:
            op = Op(e, None, False)
            op.idx = len(self.ops[e])
            for src in lasts:
                if src.eng == e and e == "pe":
                    continue
                if src.idx < self.emitted[src.eng]:
                    continue
                src.signal = True
                op.waits[("e", src.eng)] = src
            for d in dl:
                key = ("d", d.eng, d.dsem)
                op.waits[key] = max(op.waits.get(key, 0), d.dval)
            self.ops[e].append(op)

    def flush(self, final=False):
        nc = self.nc
        self.barrier()
        self.state = {}
        esem, dsem = self.esem, self.dsem
        for e in ENGS:
            c = self.sigcnt[e]
            for op in self.ops[e][self.emitted[e]:]:
                if op.signal and not op.dma and op.fn is not None:
                    c += 1
                op.cnt = c
            self.sigcnt[e] = c
        engmap = {"pe": "tensor", "act": "scalar", "dve": "vector", "pool": "gpsimd", "sp": "sync"}
        with nc.Block() as block:
            def make(e):
                def body(engobj):
                    seen = self.seen[e]
                    for op in self.ops[e][self.emitted[e]:]:
                        for key, v in op.waits.items():
                            if key[0] == "e":
                                val = v.cnt
                                sem = esem[key[1]]
                            else:
                                val = v
                                sem = dsem[(key[1], key[2])]
                            if seen.get(key, 0) >= val:
                                continue
                            seen[key] = val
                            engobj.wait_ge(sem, val)
                        if op.fn is None:
                            continue
                        ins = op.fn(engobj)
                        if op.dma:
                            ins.then_inc(dsem[(e, op.dsem)], 1 if op.dsem == "cc" else 16)
                        elif op.signal:
                            ins.then_inc(esem[e], 1)
                    if final:
                        for i in list(range(N_DMA_SEMS)) + ["cc"]:
                            k = (e, i)
                            if k in self.dma_cnt and seen.get(("d", e, i), 0) < self.dma_cnt[k]:
                                engobj.wait_ge(dsem[k], self.dma_cnt[k])
                    self.emitted[e] = len(self.ops[e])
                return body
            for e in ENGS:
                if len(self.ops[e]) == self.emitted[e]:
                    continue
                getattr(block, engmap[e])(make(e))


class Rec:
    def __init__(self):
        self.l = []

    def add(self, *a, **k):
        self.l.append((a, k))


class Prog:
    def __init__(self, name):
        self.nc = bass.Bass("TRN2", target_bir_lowering=False)
        self.es0 = contextlib.ExitStack()
        self.S = Sched(self.nc, self.es0)
        self.ins = {}
        self.outs = {}
        self.tog = 0
        self.bank_ptr = 0
        self.wslot = 0
        self.phase = 0

    def uniq(self, n):
        return "%s_ph%d" % (n, self.phase)

    def inp(self, name, shape, dt=F32):
        self.ins[name] = self.nc.dram_tensor(name, list(shape), dt, kind="ExternalInput").ap()
        return self.ins[name]

    def out(self, name, shape, dt=F32):
        self.outs[name] = self.nc.dram_tensor(name, list(shape), dt, kind="ExternalOutput").ap()
        return self.outs[name]

    def evac_eng(self):
        self.tog ^= 1
        return "act" if self.tog else "dve"

    def alloc_banks(self, n):
        if self.bank_ptr % n:
            self.bank_ptr += n - self.bank_ptr % n
        b = [(self.bank_ptr + i) % self.nbanks for i in range(n)]
        self.bank_ptr = (self.bank_ptr + n) % self.nbanks
        return b


def copy_op(P, eng, out, in_, r, w, scale=None):
    S = P.S
    if eng == "act":
        if scale is None:
            S.add("act", lambda e: e.activation(out=out, in_=in_, func=AF.Copy), r=r, w=w)
        else:
            S.add("act", lambda e: e.activation(out=out, in_=in_, func=AF.Copy, scale=scale), r=r, w=w)
    else:
        if scale is None:
            S.add("dve", lambda e: e.tensor_copy(out=out, in_=in_), r=r, w=w)
        else:
            S.add("dve", lambda e: e.tensor_scalar(out=out, in0=in_, scalar1=scale, scalar2=None, op0=ALU.mult), r=r, w=w)


NWS = 3


def gemm(P, form, xsrc, xtok, K, W, n0, n1, nsub, evac, cache=None):
    gemm_multi(P, form, [(xsrc, xtok, evac)], K, W, n0, n1, nsub, cache)


def gemm_multi(P, form, streams, K, W, n0, n1, nsub, cache=None):
    S = P.S
    ttn = nsub * 128
    nt = 0
    for c0 in range(n0, n1, 512):
        nw = min(512, n1 - c0)
        nj = nsub if form == "A" else (nw + 127) // 128
        bankss = [P.alloc_banks(4) for _ in streams]
        for k0 in range(0, K, 16):
            kc = min(16, K - k0)
            slot = P.wslot
            P.wslot = (P.wslot + 1) % NWS
            wt = P.wt[slot]
            if cache is not None and cache["mode"] == "use":
                cv = cache["dram"][cache["i"]][:, 0:kc * nw].rearrange("p (c n) -> p c n", n=nw)
                cache["i"] += 1
                S.add("pool", lambda e, wt=wt, kc=kc, nw=nw, cv=cv: e.dma_start(out=wt[:, 0:kc, 0:nw], in_=cv),
                      w=["w%d" % slot], dma=True)
            else:
                S.add("pool", lambda e, wt=wt, k0=k0, kc=kc, c0=c0, nw=nw: e.dma_start(
                    out=wt[:, 0:kc, 0:nw],
                    in_=W[k0 * 128:(k0 + kc) * 128, c0:c0 + nw].rearrange("(c p) n -> p c n", p=128)),
                    w=["w%d" % slot], dma=True)
                if cache is not None:
                    cv = cache["dram"][cache["i"]][:, 0:kc * nw].rearrange("p (c n) -> p c n", n=nw)
                    cache["i"] += 1
                    S.add("sp", lambda e, wt=wt, kc=kc, nw=nw, cv=cv: e.dma_start(out=cv, in_=wt[:, 0:kc, 0:nw]),
                          r=["w%d" % slot], dma=True)
            for si, (xsrc, xtok, evac) in enumerate(streams):
                banks = bankss[si]
                for j in range(nj):
                    bk = P.bank[banks[j]]
                    for k in range(kc):
                        kk = k0 + k
                        if form == "A":
                            S.add("pe", lambda e, bk=bk, wt=wt, k=k, kk=kk, j=j, nw=nw, xsrc=xsrc: e.matmul(
                                bk[:, 0:nw], lhsT=xsrc(kk)[:, j * 128:(j + 1) * 128], rhs=wt[:, k, 0:nw],
                                start=(kk == 0), stop=(kk == K - 1)),
                                r=["w%d" % slot, xtok(kk)], w=["pb%d" % banks[j]])
                        else:
                            S.add("pe", lambda e, bk=bk, wt=wt, k=k, kk=kk, j=j, xsrc=xsrc: e.matmul(
                                bk[:, 0:ttn], lhsT=wt[:, k, j * 128:(j + 1) * 128], rhs=xsrc(kk)[:, 0:ttn],
                                start=(kk == 0), stop=(kk == K - 1)),
                                r=["w%d" % slot, xtok(kk)], w=["pb%d" % banks[j]])
                if k0 + 16 >= K:
                    for j in range(nj):
                        evac(nt, j, P.bank[banks[j]], "pb%d" % banks[j], nw)
        nt += 1


def transpose_tm_to_fm(P, src, srctok, dst_fn, dsttok_fn, nfeat=D):
    S = P.S
    for c4 in range(nfeat // 512):
        b = P.alloc_banks(1)[0]
        bk = P.bank[b]
        for i in range(4):
            c = c4 * 4 + i
            S.add("pe", lambda e, bk=bk, i=i, c=c: e.transpose(
                out=bk[:, i * 128:(i + 1) * 128], in_=src[:, c * 128:(c + 1) * 128], identity=P.identf[:, :]),
                r=[srctok, "identf"], w=["pb%d" % b])
        copy_op(P, P.evac_eng(), dst_fn(c4), bk[:, :].rearrange("p (a t) -> p a t", a=4),
                r=["pb%d" % b], w=dsttok_fn(c4))


WSEL = 3588


def p_dram(P, l):
    nc = P.nc
    D_ = lambda n, sh, dt: nc.dram_tensor("%s_l%d" % (n, l), sh, dt, kind="Internal").ap()
    return {
        "qk": D_("pqk", [1024, SEQ], F32),
        "vm": D_("pvm", [SEQ, 512], BF16),
        "om": D_("pom", [SEQ, 512], F32),
        "gt": D_("pgt", [SEQ, 4], F32),
        "qa": D_("pqa", [512, SEQ], BF16),
        "ka": D_("pka", [512, SEQ], BF16),
        "va": D_("pva", [SEQ, 512], BF16),
    }


def p_stage(P, w_in, o, toks, stf, stb):
    S = P.S
    cnt = {"f": 0, "b": 0}

    def stage(kind):
        i = cnt[kind]
        cnt[kind] += 1
        return (stf(i) if kind == "f" else stb(i))

    def evB(dst, kind, tok0, scale=None):
        def f(nt, j, bk, btok, nw):
            ap, tk = stage(kind)
            copy_op(P, P.evac_eng(), ap[:, 0:TT], bk[:, 0:TT], r=[btok], w=[tk], scale=scale)
            r0 = nt * 512 + j * 128
            S.add("sp", lambda e: e.dma_start(out=dst[r0:r0 + 128, tok0:tok0 + TT], in_=ap[:, 0:TT]), r=[tk], dma=True)
        return f

    def evA(dst, kind, tok0, func=None):
        def f(nt, j, bk, btok, nw):
            ap, tk = stage(kind)
            if func is None:
                copy_op(P, P.evac_eng(), ap[:, 0:nw], bk[:, 0:nw], r=[btok], w=[tk])
            else:
                S.add("act", lambda e: e.activation(out=ap[:, 0:nw], in_=bk[:, 0:nw], func=func), r=[btok], w=[tk])
            t0 = tok0 + j * 128
            S.add("sp", lambda e: e.dma_start(out=dst[t0:t0 + 128, 0:nw], in_=ap[:, 0:nw]), r=[tk], dma=True)
        return f

    def run(form, n0, n1, mk):
        gemm_multi(P, form, [(xsrc, xtok, mk(tok0)) for (tok0, xsrc, xtok) in toks], 16, w_in, n0, n1, 4)

    run("B", 0, 1024, lambda t: evB(o["qk"], "f", t))
    run("A", 1024, 1536, lambda t: evA(o["vm"], "b", t))
    run("A", 1536, 2048, lambda t: evA(o["om"], "f", t, AF.Sigmoid))
    run("A", 2048, 2052, lambda t: evA(o["gt"], "f", t))
    run("B", 2052, 2564, lambda t: evB(o["qa"], "b", t, scale=128.0 ** -0.5))
    run("B", 2564, 3076, lambda t: evB(o["ka"], "b", t))
    run("A", 3076, 3588, lambda t: evA(o["va"], "b", t))


def setup_common(P, es, G):
    nc, S = P.nc, P.S
    P.nbanks = 8
    P.bank_ptr = 0
    P.bank = [es.enter_context(nc.psum_tensor(P.uniq("pb%d" % i), [128, 512], F32)) for i in range(8)]
    P.wt = [es.enter_context(nc.sbuf_tensor(P.uniq("wt%d" % i), [128, 16, 512], BF16)) for i in range(NWS)]
    P.identf = es.enter_context(nc.sbuf_tensor(P.uniq("identf_s"), [128, 128], F32))
    S.add("sp", lambda e: e.dma_start(out=P.identf[:, :], in_=G["identf"]), w=["identf"], dma=True)


def p_phase(P, l, G, o, x_tm=None, xrecv=None):
    nc, S = P.nc, P.S
    P.phase += 1
    w_in = G["w_in"][l]
    with contextlib.ExitStack() as es:
        setup_common(P, es, G)
        T = lambda n, sh, dt=F32: es.enter_context(nc.sbuf_tensor(P.uniq(n), sh, dt))
        xT = T("xT", [128, 4, 16, TT], BF16)
        stf_t = T("stf", [128, 4, 512])
        stb_t = T("stb", [128, 4, 512], BF16)
        stf = lambda i: (stf_t[:, i % 4, :], "stf%d" % (i % 4))
        stb = lambda i: (stb_t[:, i % 4, :], "stb%d" % (i % 4))
        if x_tm is not None:
            xres = T("xres", [128, 4, D])
        def load_tile(tt):
            tok0 = tt * TT
            xb = tt % 4
            if x_tm is not None:
                for ts in range(4):
                    S.add("sp", lambda e, ts=ts, tok0=tok0: e.dma_start(
                        out=xres[:, ts, :], in_=x_tm[tok0 + ts * 128: tok0 + (ts + 1) * 128, :]), w=["xres%d" % ts], dma=True)
                    transpose_tm_to_fm(P, xres[:, ts, :], "xres%d" % ts,
                                       lambda c4, ts=ts, xb=xb: xT[:, xb, c4 * 4:(c4 + 1) * 4, ts * 128:(ts + 1) * 128],
                                       lambda c4, xb=xb: ["xT%d_%d" % (xb, c4 * 4 + i) for i in range(4)])
            else:
                S.add("sp", lambda e, tt=tt, xb=xb: e.dma_start(
                    out=xT[:, xb, :, :], in_=xrecv[tt % 4, tt // 4].rearrange("(c p) t -> p c t", p=128)),
                    w=["xT%d_%d" % (xb, k) for k in range(16)], dma=True)
            return (tok0, (lambda k, xb=xb: xT[:, xb, k, :]), (lambda k, xb=xb: "xT%d_%d" % (xb, k)))

        npair = SEQ // TT // 2
        pend = [load_tile(0), load_tile(1)]
        for pr in range(npair):
            cur = pend
            if pr + 1 < npair:
                pend = [load_tile(2 * pr + 2), load_tile(2 * pr + 3)]
            p_stage(P, w_in, o, cur, stf, stb)
        S.flush()


def r_phase(P, l, G, x_in, yrecv, xout, xsend, xrecv, wcache):
    nc, S = P.nc, P.S
    P.phase += 1
    cache = {"dram": wcache, "i": 0, "mode": "fill"}
    mem = G["mem"]
    w_out, xq, xk, xv, xo, w1, w3, w2 = (G[k][l] for k in ("w_out", "xq", "xk", "xv", "xo", "w1", "w3", "w2"))
    lnp = G["lnp"][l]
    with contextlib.ExitStack() as es:
        setup_common(P, es, G)
        T = lambda n, sh, dt=F32: es.enter_context(nc.sbuf_tensor(P.uniq(n), sh, dt))
        xres = T("xres", [128, 4, D])
        xT = T("xT", [128, 16, TT], BF16)
        big = T("big", [128, 44, TT], BF16)
        kmT = T("kmT", [128, 16, NMEM], BF16)
        vmm = T("vmm", [128, 2, D], BF16)
        lng = T("lng", [128, D])
        lnb = T("lnb", [128, D])
        onesb = T("onesb_s", [128, 128], BF16)
        hsel = T("hsel_s", [128, 2])
        pT = T("pT", [128, 2, 2, TT], BF16)
        rz = T("rz", [128, 2, TT])
        sl = T("sl", [128, 4, TT], BF16)
        st = T("st", [128, 4, 4, 6])
        mv = T("mv", [128, 4, 2])
        rstd = T("rstd", [128, 4, 1])
        nmr = T("nmr", [128, 4, 1])
        lnT = T("lnT", [128, 6, 16])
        S.add("sp", lambda e: e.dma_start(out=lnT[:, :, :], in_=G["lnpT"][l].rearrange("p (a c) -> p a c", c=16)), w=["lnT"], dma=True)
        S.add("sp", lambda e: e.dma_start(out=onesb[:, :], in_=G["onesb"]), w=["onesb"], dma=True)
        S.add("sp", lambda e: e.dma_start(out=hsel[:, :], in_=G["hsel"]), w=["hsel"], dma=True)

        for mt in range(2):
            S.add("sp", lambda e, mt=mt: e.dma_start(out=xres[:, mt, :], in_=mem[mt * 128:(mt + 1) * 128, :]),
                  w=["xres%d" % mt], dma=True)
            transpose_tm_to_fm(P, xres[:, mt, :], "xres%d" % mt,
                               lambda c4, mt=mt: xT[:, c4 * 4:(c4 + 1) * 4, mt * 128:(mt + 1) * 128],
                               lambda c4: ["xT%d" % (c4 * 4 + i) for i in range(4)])

        def ev_k(nt, j, bk, btok, nw):
            copy_op(P, P.evac_eng(), kmT[:, nt * 4 + j, :], bk[:, 0:NMEM], r=[btok], w=["kmT%d" % (nt * 4 + j)])
        gemm(P, "B", lambda k: xT[:, k, :], lambda k: "xT%d" % k, 16, xk, 0, D, 2, ev_k)

        def ev_v(nt, j, bk, btok, nw):
            copy_op(P, P.evac_eng(), vmm[:, j, nt * 512:(nt + 1) * 512], bk[:, 0:512], r=[btok], w=["vmm"])
        gemm(P, "A", lambda k: xT[:, k, :], lambda k: "xT%d" % k, 16, xv, 0, D, 2, ev_v)

        def ev_resid(nt, j, bk, btok, nw):
            S.add("dve", lambda e: e.scalar_tensor_tensor(
                out=xres[:, j, nt * 512:(nt + 1) * 512], in0=xres[:, j, nt * 512:(nt + 1) * 512], scalar=float(ALPHA),
                in1=bk[:, 0:512], op0=ALU.mult, op1=ALU.add), r=[btok, "xres%d" % j], w=["xres%d" % j])

        def layernorm(li):
            S.add("sp", lambda e: e.dma_start(out=lng[:, :], in_=lnp[2 * li:2 * li + 1, :].partition_broadcast(128)), w=["lng"], dma=True)
            S.add("sp", lambda e: e.dma_start(out=lnb[:, :], in_=lnp[2 * li + 1:2 * li + 2, :].partition_broadcast(128)), w=["lnb"], dma=True)
            for ts in range(4):
                xt = "xres%d" % ts
                for q in range(4):
                    S.add("dve", lambda e, ts=ts, q=q: e.bn_stats(out=st[:, ts, q, :], in_=xres[:, ts, q * 512:(q + 1) * 512]),
                          r=[xt], w=["st%d" % ts])
                S.add("dve", lambda e, ts=ts: e.bn_aggr(out=mv[:, ts, :], in_=st[:, ts, :, :].rearrange("p a b -> p (a b)")),
                      r=["st%d" % ts], w=["mv%d" % ts])
            for ts in range(4):
                S.add("act", lambda e, ts=ts: e.activation(out=rstd[:, ts, :], in_=mv[:, ts, 1:2], func=AF.Sqrt, bias=float(EPS)),
                      r=["mv%d" % ts], w=["rstd%d" % ts])
            for ts in range(4):
                S.add("dve", lambda e, ts=ts: e.reciprocal(out=rstd[:, ts, :], in_=rstd[:, ts, :]), r=["rstd%d" % ts], w=["rstd%d" % ts])
                S.add("dve", lambda e, ts=ts: e.tensor_scalar(
                    out=nmr[:, ts, :], in0=mv[:, ts, 0:1], scalar1=rstd[:, ts, 0:1], scalar2=-1.0, op0=ALU.mult, op1=ALU.mult),
                    r=["mv%d" % ts, "rstd%d" % ts], w=["nmr%d" % ts])
            for ts in range(4):
                xt = "xres%d" % ts
                S.add("act", lambda e, ts=ts: e.activation(
                    out=xres[:, ts, :], in_=xres[:, ts, :], func=AF.Identity, scale=rstd[:, ts, 0:1], bias=nmr[:, ts, 0:1]),
                    r=[xt, "nmr%d" % ts, "rstd%d" % ts], w=[xt])
            for ts in range(4):
                xt = "xres%d" % ts
                for c4 in range(4):
                    b = P.alloc_banks(1)[0]
                    bk = P.bank[b]
                    for i in range(4):
                        c = c4 * 4 + i
                        S.add("pe", lambda e, bk=bk, i=i, c=c, ts=ts: e.transpose(
                            out=bk[:, i * 128:(i + 1) * 128], in_=xres[:, ts, c * 128:(c + 1) * 128], identity=P.identf[:, :]),
                            r=[xt, "identf"], w=["pb%d" % b])
                    for i in range(4):
                        c = c4 * 4 + i
                        if P.evac_eng() == "act":
                            S.add("act", lambda e, bk=bk, i=i, c=c, ts=ts: e.activation(
                                out=xT[:, c, ts * 128:(ts + 1) * 128], in_=bk[:, i * 128:(i + 1) * 128], func=AF.Identity,
                                scale=lnT[:, 2 * li, c:c + 1], bias=lnT[:, 2 * li + 1, c:c + 1]),
                                r=["pb%d" % b, "lnT"], w=["xT%d" % c])
                        else:
                            S.add("dve", lambda e, bk=bk, i=i, c=c, ts=ts: e.tensor_scalar(
                                out=xT[:, c, ts * 128:(ts + 1) * 128], in0=bk[:, i * 128:(i + 1) * 128],
                                scalar1=lnT[:, 2 * li, c:c + 1], scalar2=lnT[:, 2 * li + 1, c:c + 1], op0=ALU.mult, op1=ALU.add),
                                r=["pb%d" % b, "lnT"], w=["xT%d" % c])
            for ts in range(4):
                xt = "xres%d" % ts
                S.add("dve", lambda e, ts=ts: e.tensor_tensor(out=xres[:, ts, :], in0=xres[:, ts, :], in1=lng[:, :], op=ALU.mult),
                      r=[xt, "lng"], w=[xt])
                S.add("dve", lambda e, ts=ts: e.tensor_tensor(out=xres[:, ts, :], in0=xres[:, ts, :], in1=lnb[:, :], op=ALU.add),
                      r=[xt, "lnb"], w=[xt])

        xsrc = lambda k: xT[:, k, :]
        xtok = lambda k: "xT%d" % k
        bsrc = lambda k: big[:, k, :]
        btokf = lambda k: "big%d" % k

        pend_cc = []
        for tt in range(4):
            tok0 = tt * TT
            cache["i"] = 0
            cache["mode"] = "fill" if tt == 0 else "use"
            for ts in range(4):
                S.add("sp", lambda e, ts=ts, tok0=tok0: e.dma_start(
                    out=xres[:, ts, :], in_=x_in[tok0 + ts * 128: tok0 + (ts + 1) * 128, :]), w=["xres%d" % ts], dma=True)
            for q in range(2):
                ra = 16 if q == 0 else 32
                rb = 24 if q == 0 else 16
                for (reg, half) in ((ra, 0), (rb, 1)):
                    tcc = (half * 2048 + tok0) // 1024
                    off = (half * 2048 + tok0) % 1024
                    S.add("sp", lambda e, reg=reg, q=q, tcc=tcc, off=off: e.dma_start(
                        out=big[:, reg:reg + 8, :],
                        in_=yrecv[tcc, q, :, off:off + TT].rearrange("(c p) t -> p c t", p=128)),
                        r=["yrecv%d" % tcc], w=["big%d" % k for k in range(reg, reg + 8)], dma=True)
                for gq in range(2):
                    d0 = 8 * q + 4 * gq
                    a0 = ra + 4 * gq
                    b0 = rb + 4 * gq
                    S.add("act", lambda e, d0=d0, a0=a0: e.activation(
                        out=big[:, d0:d0 + 4, :], in_=big[:, a0:a0 + 4, :], func=AF.Copy, scale=hsel[:, 0:1]),
                        r=["big%d" % k for k in range(a0, a0 + 4)] + ["hsel"], w=["big%d" % k for k in range(d0, d0 + 4)])
                    S.add("dve", lambda e, d0=d0, b0=b0: e.scalar_tensor_tensor(
                        out=big[:, d0:d0 + 4, :], in0=big[:, b0:b0 + 4, :], scalar=hsel[:, 1:2], in1=big[:, d0:d0 + 4, :],
                        op0=ALU.mult, op1=ALU.add),
                        r=["big%d" % k for k in range(b0, b0 + 4)] + ["hsel"] + ["big%d" % k for k in range(d0, d0 + 4)],
                        w=["big%d" % k for k in range(d0, d0 + 4)])
            gemm(P, "A", bsrc, btokf, 16, w_out, 0, D, 4, ev_resid, cache)
            while pend_cc:
                t_ = pend_cc.pop(0)
                allgather_chunk(P, xsend[t_], xrecv[t_], r=["xsend%d" % t_])
            layernorm(0)
            def ev_q(nt, j, bk, btok, nw):
                copy_op(P, P.evac_eng(), big[:, 16 + nt * 4 + j, :], bk[:, 0:TT], r=[btok], w=["big%d" % (16 + nt * 4 + j)])
            gemm(P, "B", xsrc, xtok, 16, xq, 0, D, 4, ev_q, cache)
            def att_scores(h):
                ps = h % 2
                for mt in range(2):
                    b = P.alloc_banks(1)[0]
                    bk = P.bank[b]
                    for dc in range(4):
                        S.add("pe", lambda e, bk=bk, h=h, dc=dc, mt=mt: e.matmul(
                            bk[:, 0:TT], lhsT=kmT[:, 4 * h + dc, mt * 128:(mt + 1) * 128], rhs=big[:, 16 + 4 * h + dc, :],
                            start=(dc == 0), stop=(dc == 3)),
                            r=["kmT%d" % (4 * h + dc), "big%d" % (16 + 4 * h + dc)], w=["pb%d" % b])
                    S.add("act", lambda e, bk=bk, ps=ps, mt=mt: e.activation(
                        out=pT[:, ps, mt, :], in_=bk[:, 0:TT], func=AF.Exp, scale=512.0 ** -0.5),
                        r=["pb%d" % b], w=["pT%d%d" % (ps, mt)])

            def att_zo(h):
                ps = h % 2
                bz = P.alloc_banks(1)[0]
                for mt in range(2):
                    S.add("pe", lambda e, bz=bz, ps=ps, mt=mt: e.matmul(
                        P.bank[bz][:, 0:TT], lhsT=onesb[:, :], rhs=pT[:, ps, mt, :], start=(mt == 0), stop=(mt == 1)),
                        r=["onesb", "pT%d%d" % (ps, mt)], w=["pb%d" % bz])
                S.add("dve", lambda e, bz=bz, ps=ps: e.reciprocal(out=rz[:, ps, :], in_=P.bank[bz][:, 0:TT]),
                      r=["pb%d" % bz], w=["rz%d" % ps])
                for ec in range(4):
                    b = P.alloc_banks(1)[0]
                    bk = P.bank[b]
                    for mt in range(2):
                        S.add("pe", lambda e, bk=bk, h=h, ec=ec, mt=mt, ps=ps: e.matmul(
                            bk[:, 0:TT], lhsT=vmm[:, mt, 512 * h + ec * 128: 512 * h + (ec + 1) * 128], rhs=pT[:, ps, mt, :],
                            start=(mt == 0), stop=(mt == 1)),
                            r=["vmm", "pT%d%d" % (ps, mt)], w=["pb%d" % b])
                    S.add("dve", lambda e, bk=bk, h=h, ec=ec, ps=ps: e.tensor_tensor(
                        out=big[:, 4 * h + ec, :], in0=bk[:, 0:TT], in1=rz[:, ps, :], op=ALU.mult),
                        r=["pb%d" % b, "rz%d" % ps], w=["big%d" % (4 * h + ec)])
            if "b" in OPT:
                att_scores(0)
                att_scores(1)
                att_zo(0)
                att_scores(2)
                att_zo(1)
                att_scores(3)
                att_zo(2)
                att_zo(3)
            else:
                for h_ in range(4):
                    att_scores(h_)
                    att_zo(h_)
            gemm(P, "A", bsrc, btokf, 16, xo, 0, D, 4, ev_resid, cache)
            layernorm(1)
            for nt in range(FFN // 512):
                def ev_1(_, j, bk, btok, nw):
                    S.add("act", lambda e: e.activation(out=sl[:, j, :], in_=bk[:, 0:TT], func=AF.Silu), r=[btok], w=["sl%d" % j])
                gemm(P, "B", xsrc, xtok, 16, w1, nt * 512, (nt + 1) * 512, 4, ev_1, cache)

                def ev_3(_, j, bk, btok, nw, nt=nt):
                    S.add("dve", lambda e: e.tensor_tensor(out=big[:, nt * 4 + j, :], in0=bk[:, 0:TT], in1=sl[:, j, :], op=ALU.mult),
                          r=[btok, "sl%d" % j], w=["big%d" % (nt * 4 + j)])
                gemm(P, "B", xsrc, xtok, 16, w3, nt * 512, (nt + 1) * 512, 4, ev_3, cache)
            gemm(P, "A", bsrc, btokf, 44, w2, 0, D, 4, ev_resid, cache)
            layernorm(2)
            for ts in range(4):
                S.add("sp", lambda e, ts=ts, tok0=tok0: e.dma_start(
                    out=xout[tok0 + ts * 128: tok0 + (ts + 1) * 128, :], in_=xres[:, ts, :]), r=["xres%d" % ts], dma=True)
            if xsend is not None:
                S.add("sp", lambda e, tt=tt: e.dma_start(
                    out=xsend[tt].rearrange("(c p) t -> p c t", p=128), in_=xT[:, :, :]),
                    r=["xT%d" % k for k in range(16)], w=["xsend%d" % tt], dma=True)
                pend_cc.append(tt)
        while pend_cc:
            t_ = pend_cc.pop(0)
            allgather_chunk(P, xsend[t_], xrecv[t_], r=["xsend%d" % t_])
        S.flush()


PATS = ((1, 32), (4, 8), (16, 2))


def m_phase(P, l, o, G, yT_out):
    nc, S = P.nc, P.S
    qk_in, vm_in, om_in, gt_in, qa_in, ka_in, va_in = o["qk"], o["vm"], o["om"], o["gt"], o["qa"], o["ka"], o["va"]
    cw_in, cb_in, gbb_in, mhg_in = G["cw"][l], G["cb"][l], G["gbb"][l], G["mhg"][l]
    biasT_in, mask_in, identb_in, onesb_in, u01_in, onesf_in = G["biasT"], G["maskc"], G["identb"], G["onesb"], G["u01"], G["onesf"]

    with contextlib.ExitStack() as esP:
        with contextlib.ExitStack() as es:
            T = lambda n, sh, dt=F32: es.enter_context(nc.sbuf_tensor(P.uniq(n), sh, dt))
            P.nbanks = 8
            P.bank_ptr = 0
            P.bank = [es.enter_context(nc.psum_tensor(P.uniq("pa%d" % i), [128, 512], F32)) for i in range(8)]
            identb = T("identb_s", [128, 128], BF16)
            onesb = T("onesb_s", [128, 128], BF16)
            maskc = T("maskc_s", [128, 3, 256])
            S.add("sp", lambda e: e.dma_start(out=identb[:, :], in_=identb_in), w=["identb"], dma=True)
            S.add("sp", lambda e: e.dma_start(out=onesb[:, :], in_=onesb_in), w=["onesb"], dma=True)
            S.add("sp", lambda e: e.dma_start(out=maskc[:, :, :], in_=mask_in), w=["maskc"], dma=True)
            qTs = [[T("qT%d_%d" % (i, a), [128, SEQ], BF16) for i in range(3)] for a in range(2)]
            kTs = [[T("kT%d_%d" % (i, a), [128, SEQ], BF16) for i in range(3)] for a in range(2)]
            vt_single = [T("vt%d" % i, [128, 32, 128], BF16) for i in range(3)]
            vts = [vt_single, vt_single]
            acc = T("acc", [128, 2, SEQ])
            yTa = T("yTa", [128, SEQ], BF16)
            biasfs = [T("biasf%d" % a, [128, 3, 256]) for a in range(2)]
            bms = [T("bm%d" % a, [128, 3, 256], BF16) for a in range(2)]
            pTt = T("pTt", [128, 4, 256], BF16)
            def head_loads(h):
                hp = h % 2
                qT, kT, vt, bm, biasf = qTs[hp], kTs[hp], vts[hp], bms[hp], biasfs[hp]
                hs = slice(h * 128, (h + 1) * 128)
                S.add("sp", lambda e, qT=qT, kT=kT, vt=vt, bm=bm, biasf=biasf, hs=hs: e.dma_start(out=qT[0][:, :], in_=qa_in[hs, :]), w=["qT0_%d" % hp], dma=True)
                S.add("sp", lambda e, qT=qT, kT=kT, vt=vt, bm=bm, biasf=biasf, hs=hs: e.dma_start(out=kT[0][:, :], in_=ka_in[hs, :]), w=["kT0_%d" % hp], dma=True)
                S.add("sp", lambda e, qT=qT, kT=kT, vt=vt, bm=bm, biasf=biasf, h=h: e.dma_start(out=biasf[:, :, :], in_=biasT_in[h]), w=["biasf_%d" % hp], dma=True)
                S.add("dve", lambda e, qT=qT, kT=kT, vt=vt, bm=bm, biasf=biasf: e.tensor_tensor(out=bm[:, :, :], in0=biasf[:, :, :], in1=maskc[:, :, :], op=ALU.add),
                      r=["biasf_%d" % hp, "maskc"], w=["bm_%d" % hp])
                for pi in (1, 2):
                    dil, nbk = PATS[pi]
                    S.add("pool", lambda e, qT=qT, kT=kT, vt=vt, bm=bm, biasf=biasf, pi=pi, dil=dil: e.tensor_copy(
                        out=qT[pi][:, :].rearrange("p (r q) -> p r q", r=dil),
                        in_=qT[0][:, :].rearrange("p (q r) -> p r q", r=dil)), r=["qT0_%d" % hp], w=["qT%d_%d" % (pi, hp)])
                    S.add("pool", lambda e, qT=qT, kT=kT, vt=vt, bm=bm, biasf=biasf, pi=pi, dil=dil: e.tensor_copy(
                        out=kT[pi][:, :].rearrange("p (r q) -> p r q", r=dil),
                        in_=kT[0][:, :].rearrange("p (q r) -> p r q", r=dil)), r=["kT0_%d" % hp], w=["kT%d_%d" % (pi, hp)])

            def head_units(h):
                hp = h % 2
                qT, kT, vt, bm, biasf = qTs[hp], kTs[hp], vts[hp], bms[hp], biasfs[hp]
                hs = slice(h * 128, (h + 1) * 128)
                S.add("sp", lambda e, qT=qT, kT=kT, vt=vt, bm=bm, biasf=biasf, hs=hs: e.dma_start(
                    out=vt[0][:, :, :], in_=va_in[:, hs].rearrange("(n i) e -> i n e", i=128)), w=["vt0"], dma=True)
                for pi in (1, 2):
                    dil, nbk = PATS[pi]
                    for rho in range(dil):
                        S.add("sp", lambda e, qT=qT, kT=kT, vt=vt, bm=bm, biasf=biasf, hs=hs, pi=pi, dil=dil, nbk=nbk, rho=rho: e.dma_start(
                            out=vt[pi][:, rho * nbk:(rho + 1) * nbk, :],
                            in_=va_in[:, hs].rearrange("(n i r) e -> r i n e", i=128, r=dil)[rho]), w=["vt%d" % pi], dma=True)
                ucnt = 0
                for pi in range(3):
                    dil, nbk = PATS[pi]
                    accv = acc[:, :, :].rearrange("p t (n i r) -> p t r n i", i=128, r=dil)
                    for rho in range(dil):
                        for n in range(nbk):
                            u = rho * nbk + n
                            col = u * 128
                            pb_ = ucnt % (4 if 'c' in OPT else 2)
                            ucnt += 1
                            bs = P.alloc_banks(1)[0]
                            sps = P.bank[bs]
                            lo = 0 if n > 0 else 128
                            S.add("pe", lambda e, qT=qT, kT=kT, vt=vt, bm=bm, biasf=biasf, sps=sps, pi=pi, lo=lo: e.matmul(
                                sps[:, lo:256], lhsT=identb[:, :], rhs=bm[:, pi, lo:256], start=True, stop=False),
                                r=["identb", "bm_%d" % hp], w=["pb%d" % bs])
                            if n > 0:
                                S.add("pe", lambda e, qT=qT, kT=kT, vt=vt, bm=bm, biasf=biasf, sps=sps, pi=pi, col=col: e.matmul(
                                    sps[:, 0:128], lhsT=kT[pi][:, col - 128:col], rhs=qT[pi][:, col:col + 128], start=False, stop=False),
                                    r=["kT%d_%d" % (pi, hp), "qT%d_%d" % (pi, hp)], w=["pb%d" % bs])
                            S.add("pe", lambda e, qT=qT, kT=kT, vt=vt, bm=bm, biasf=biasf, sps=sps, pi=pi, col=col: e.matmul(
                                sps[:, 128:256], lhsT=kT[pi][:, col:col + 128], rhs=qT[pi][:, col:col + 128], start=False, stop=True),
                                r=["kT%d_%d" % (pi, hp), "qT%d_%d" % (pi, hp)], w=["pb%d" % bs])
                            S.add("act", lambda e, qT=qT, kT=kT, vt=vt, bm=bm, biasf=biasf, sps=sps, pb_=pb_, lo=lo: e.activation(
                                out=pTt[:, pb_, lo:256], in_=sps[:, lo:256], func=AF.Exp), r=["pb%d" % bs], w=["pTt%d" % pb_])
                            bu = P.alloc_banks(1)[0]
                            ups = P.bank[bu]
                            if n > 0:
                                S.add("pe", lambda e, qT=qT, kT=kT, vt=vt, bm=bm, biasf=biasf, ups=ups, pi=pi, u=u, pb_=pb_: e.matmul(
                                    ups[:, 0:128], lhsT=vt[pi][:, u - 1, :], rhs=pTt[:, pb_, 0:128], start=True, stop=False),
                                    r=["vt%d" % pi, "pTt%d" % pb_], w=["pb%d" % bu])
                            S.add("pe", lambda e, qT=qT, kT=kT, vt=vt, bm=bm, biasf=biasf, ups=ups, pi=pi, u=u, pb_=pb_, n=n: e.matmul(
                                ups[:, 0:128], lhsT=vt[pi][:, u, :], rhs=pTt[:, pb_, 128:256], start=(n == 0), stop=True),
                                r=["vt%d" % pi, "pTt%d" % pb_], w=["pb%d" % bu])
                            if n > 0:
                                S.add("pe", lambda e, qT=qT, kT=kT, vt=vt, bm=bm, biasf=biasf, ups=ups, pb_=pb_: e.matmul(
                                    ups[:, 128:256], lhsT=onesb[:, :], rhs=pTt[:, pb_, 0:128], start=True, stop=False),
                                    r=["onesb", "pTt%d" % pb_], w=["pb%d" % bu])
                            S.add("pe", lambda e, qT=qT, kT=kT, vt=vt, bm=bm, biasf=biasf, ups=ups, pb_=pb_, n=n: e.matmul(
                                ups[:, 128:256], lhsT=onesb[:, :], rhs=pTt[:, pb_, 128:256], start=(n == 0), stop=True),
                                r=["onesb", "pTt%d" % pb_], w=["pb%d" % bu])
                            av = accv[:, :, rho, n, :]
                            uv = ups[:, 0:256].rearrange("p (t q) -> p t q", t=2)
                            if pi == 0:
                                S.add("dve", lambda e, qT=qT, kT=kT, vt=vt, bm=bm, biasf=biasf, av=av, uv=uv: e.tensor_copy(out=av, in_=uv), r=["pb%d" % bu], w=["acc"])
                            else:
                                S.add("dve", lambda e, qT=qT, kT=kT, vt=vt, bm=bm, biasf=biasf, av=av, uv=uv: e.tensor_tensor(out=av, in0=av, in1=uv, op=ALU.add),
                                      r=["pb%d" % bu, "acc"], w=["acc"])
                S.add("act", lambda e: e.activation(out=acc[:, 1, :], in_=acc[:, 1, :], func=AF.Ln), r=["acc"], w=["acc"])
                S.add("act", lambda e: e.activation(out=acc[:, 1, :], in_=acc[:, 1, :], func=AF.Exp, scale=-1.0), r=["acc"], w=["acc"])
                S.add("dve", lambda e, qT=qT, kT=kT, vt=vt, bm=bm, biasf=biasf: e.tensor_tensor(out=yTa[:, :], in0=acc[:, 0, :], in1=acc[:, 1, :], op=ALU.mult),
                      r=["acc"], w=["yTa"])
                S.add("sp", lambda e, qT=qT, kT=kT, vt=vt, bm=bm, biasf=biasf, h=h: e.dma_start(out=yT_out[:, 512 + h * 128: 512 + (h + 1) * 128, :].rearrange("c f t -> f c t"), in_=yTa[:, :].rearrange("p (c t) -> p c t", c=4)),
                      r=["yTa"], dma=True)
            head_loads(0)
            for h in range(4):
                if h + 1 < 4:
                    head_loads(h + 1)
                head_units(h)
            S.flush()
        with contextlib.ExitStack() as es:
            T = lambda n, sh, dt=F32: es.enter_context(nc.sbuf_tensor(P.uniq(n), sh, dt))
            P.nbanks = 6
            P.bank_ptr = 4
            P.bank = [es.enter_context(nc.psum_tensor(P.uniq("pb%d" % i), [128, 512], F32)) for i in range(6)]
            pbf = [es.enter_context(nc.psum_tensor(P.uniq("pbf%d" % i), [128, 1024], BF16)) for i in range(2)]
            identb = T("identb2", [128, 128], BF16)
            u01 = T("u01_s", [128, 128])
            onesf = T("onesf_s", [128, 128])
            S.add("sp", lambda e: e.dma_start(out=identb[:, :], in_=identb_in), w=["identb"], dma=True)
            S.add("sp", lambda e: e.dma_start(out=u01[:, :], in_=u01_in), w=["u01"], dma=True)
            S.add("sp", lambda e: e.dma_start(out=onesf[:, :], in_=onesf_in), w=["onesf"], dma=True)
            qkc = T("qkc", [128, 8, SEQ], BF16)
            ktm = T("ktm", [128, 32, 512], BF16)
            vext = T("vext", [128, 64, 257], BF16)
            yTm = T("yTm", [128, 4, SEQ], BF16)
            cw = T("cw_s", [128, 8, 4])
            cb = T("cb_s", [128, 8])
            mhg = T("mhg_s", [128, 512])
            gsb = T("gsb", [128, 32, 4])
            gbb = T("gbb_s", [128, 32, 4])
            spt = T("spt", [128, 64])
            t1 = T("t1", [128, 64])
            wcol = T("wcol", [128, 64])
            ebcol = T("ebcol", [128, 64])
            ebl = T("ebl", [128, 64])
            pre = T("pre", [128, 2, SEQ + 4], BF16)
            dg = T("dg", [128, 8, 4, 128], BF16)
            G32 = T("G32", [128, 2, 2, 257])
            Cbf = T("Cbf", [128, 2, 2, 257], BF16)
            S0 = T("S0", [128, 2, 128], BF16)
            vp = T("vp", [128, 2, 257], BF16)
            omb = T("omb", [128, 2, 512])
            h32 = T("h32", [128, 2, 256])
            ytm = T("ytm", [128, 2, 256], BF16)
            sm = T("sm", [128, 2, 8])
            bst = T("bst", [128, 2, 6])
            for (ap_, src_, tk) in ((cw[:, :, :], cw_in, "cw"), (cb[:, :], cb_in, "cb"), (mhg[:, :], mhg_in, "mhg")):
                S.add("sp", lambda e, ap_=ap_, src_=src_: e.dma_start(out=ap_, in_=src_), w=[tk], dma=True)
            S.add("sp", lambda e: e.dma_start(out=gbb[:, :, :], in_=gbb_in.rearrange("p (c g) -> p c g", g=4)), w=["gbb"], dma=True)
            S.add("sp", lambda e: e.dma_start(out=gsb[:, :, :], in_=gt_in.rearrange("(c p) g -> p c g", p=128)), w=["gsb"], dma=True)
            vx4 = vext[:, :, :].rearrange("p (c h) e -> p c h e", h=2)
            for hd in range(2):
                S.add("sp", lambda e, hd=hd: e.dma_start(
                    out=vx4[:, :, hd, 0:256], in_=vm_in[:, hd * 256:(hd + 1) * 256].rearrange("(c p) e -> p c e", p=128)),
                    w=["vext"], dma=True)
            S.add("pool", lambda e: e.memset(vext[:, :, 256:257], 1.0), r=["vext"], w=["vext"])
            S.add("pool", lambda e: e.memset(G32[:, :, :, :], 0.0), w=["G0", "G1"])
            S.add("pool", lambda e: e.memset(Cbf[:, :, :, :], 0.0), w=["C0", "C1"])
            S.add("dve", lambda e: e.tensor_tensor(out=gsb[:, :, :], in0=gsb[:, :, :], in1=gbb[:, :, :], op=ALU.add),
                  r=["gsb", "gbb"], w=["gsb"])
            sp3 = spt[:, :].rearrange("p (c h) -> p c h", h=2)
            S.add("act", lambda e: e.activation(out=sp3, in_=gsb[:, :, 2:4], func=AF.Exp, scale=-1.0), r=["gsb"], w=["spt"])
            S.add("act", lambda e: e.activation(out=spt[:, :], in_=spt[:, :], func=AF.Ln, bias=1.0), r=["spt"], w=["spt"])
            bcs = P.alloc_banks(1)[0]
            S.add("pe", lambda e: e.matmul(P.bank[bcs][:, 0:64], lhsT=u01[:, :], rhs=spt[:, :], start=True, stop=True),
                  r=["u01", "spt"], w=["pb%d" % bcs])
            btot = P.alloc_banks(1)[0]
            S.add("pe", lambda e: e.matmul(P.bank[btot][:, 0:64], lhsT=onesf[:, :], rhs=spt[:, :], start=True, stop=True),
                  r=["onesf", "spt"], w=["pb%d" % btot])
            S.add("dve", lambda e: e.tensor_tensor(
                out=t1[:, :].rearrange("p (c h) -> p c h", h=2), in0=gsb[:, :, 0:2],
                in1=P.bank[bcs][:, 0:64].rearrange("p (c h) -> p c h", h=2), op=ALU.add), r=["gsb", "pb%d" % bcs], w=["t1"])
            S.add("act", lambda e: e.activation(out=wcol[:, :], in_=t1[:, :], func=AF.Exp, bias=float(-math.log(16.0))), r=["t1"], w=["wcol"])
            S.add("act", lambda e: e.activation(out=ebcol[:, :], in_=P.bank[bcs][:, 0:64], func=AF.Exp, scale=-1.0),
                  r=["pb%d" % bcs], w=["ebcol"])
            S.add("act", lambda e: e.activation(out=ebl[:, :], in_=P.bank[btot][:, 0:64], func=AF.Exp, scale=-1.0),
                  r=["pb%d" % btot], w=["ebl"])
            for j in range(8):
                for tap in range(4):
                    S.add("dve", lambda e, j=j, tap=tap: e.tensor_scalar(
                        out=dg[:, j, tap, :], in0=identb[:, :], scalar1=cw[:, j, tap:tap + 1], scalar2=None, op0=ALU.mult),
                        r=["identb", "cw"], w=["dg%d" % j])
            for j in range(8):
                pb_ = j % 2
                S.add("dve", lambda e, pb_=pb_: e.memset(pre[:, pb_, 0:3], 0.0), w=["pre%d" % pb_])
                for hh in range(2):
                    S.add("pool", lambda e, j=j, pb_=pb_, hh=hh: e.dma_start(
                        out=pre[:, pb_, 3 + hh * 2048: 3 + (hh + 1) * 2048], in_=qk_in[j * 128:(j + 1) * 128, hh * 2048:(hh + 1) * 2048]),
                        w=["pre%d" % pb_], dma=True)
                for blk in range(8):
                    b = P.alloc_banks(1)[0]
                    for tap in range(4):
                        S.add("pe", lambda e, b=b, j=j, tap=tap, pb_=pb_, blk=blk: e.matmul(
                            P.bank[b][:, 0:512], lhsT=dg[:, j, tap, :], rhs=pre[:, pb_, tap + blk * 512: tap + blk * 512 + 512],
                            start=(tap == 0), stop=(tap == 3)), r=["dg%d" % j, "pre%d" % pb_], w=["pb%d" % b])
                    S.add("act", lambda e, b=b, j=j, blk=blk: e.activation(
                        out=qkc[:, j, blk * 512:(blk + 1) * 512], in_=P.bank[b][:, 0:512], func=AF.Silu, bias=cb[:, j:j + 1]),
                        r=["pb%d" % b, "cb"], w=["qkc%d" % j])
            for c in range(32):
                pb_ = c % 2
                for jj in range(4):
                    S.add("pe", lambda e, c=c, jj=jj, pb_=pb_: e.transpose(
                        out=pbf[pb_][:, jj * 128:(jj + 1) * 128], in_=qkc[:, 4 + jj, c * 128:(c + 1) * 128], identity=identb[:, :]),
                        r=["qkc%d" % (4 + jj), "identb"], w=["pbf%d" % pb_])
                copy_op(P, P.evac_eng(), ktm[:, c, :], pbf[pb_][:, 0:512], r=["pbf%d" % pb_], w=["ktm"])
            for c in range(32):
                cs_ = slice(c * 128, (c + 1) * 128)
                ob = c % 2
                S.add("sp", lambda e, cs_=cs_, ob=ob: e.dma_start(out=omb[:, ob, :], in_=om_in[cs_, :]), w=["omb%d" % ob], dma=True)
                recs = []
                S_real = S
                for hd in range(2):
                    S = Rec()
                    P.S = S
                    recs.append(S.l)
                    idx = c * 2 + hd
                    pidx = max(idx - 2, 0)
                    sb = hd
                    ba = 3 * hd
                    for dc in range(2):
                        S.add("pe", lambda e, ba=ba, hd=hd, dc=dc, cs_=cs_: e.matmul(
                            P.bank[ba][:, 0:128], lhsT=qkc[:, 4 + hd * 2 + dc, cs_], rhs=qkc[:, hd * 2 + dc, cs_],
                            start=(dc == 0), stop=(dc == 1)),
                            r=["qkc%d" % (4 + hd * 2 + dc), "qkc%d" % (hd * 2 + dc)], w=["pbA%d" % hd])
                    S.add("dve", lambda e, ba=ba, sb=sb: e.tensor_tensor(
                        out=S0[:, sb, :], in0=P.bank[ba][:, 0:128], in1=u01[:, :], op=ALU.mult),
                        r=["pbA%d" % hd, "u01"], w=["S0%d" % sb])
                    S.add("act", lambda e, idx=idx, sb=sb: e.activation(
                        out=vp[:, sb, :], in_=vext[:, idx, :], func=AF.Copy, scale=wcol[:, idx:idx + 1]),
                        r=["vext", "wcol"], w=["vp%d" % sb])
                    bacc = 3 * hd
                    S.add("pe", lambda e, bacc=bacc, sb=sb: e.matmul(
                        P.bank[bacc][:, 128:385], lhsT=S0[:, sb, :], rhs=vp[:, sb, :], start=True, stop=False),
                        r=["S0%d" % sb, "vp%d" % sb], w=["pbB%d" % hd])
                    for dc in range(2):
                        S.add("pe", lambda e, bacc=bacc, hd=hd, dc=dc, cs_=cs_: e.matmul(
                            P.bank[bacc][:, 128:385], lhsT=qkc[:, hd * 2 + dc, cs_], rhs=Cbf[:, hd, dc, :], start=False, stop=(dc == 1)),
                            r=["qkc%d" % (hd * 2 + dc), "C%d" % hd], w=["pbB%d" % hd])
                    bd = [3 * hd + 1, 3 * hd + 2]
                    for dc in range(2):
                        S.add("pe", lambda e, bd=bd, hd=hd, dc=dc, c=c, sb=sb: e.matmul(
                            P.bank[bd[dc]][:, 0:257], lhsT=ktm[:, c, hd * 256 + dc * 128: hd * 256 + (dc + 1) * 128], rhs=vp[:, sb, :],
                            start=True, stop=True), r=["ktm", "vp%d" % sb], w=["pb%d" % bd[dc]])
                    for dc in range(2):
                        S.add("dve", lambda e, bd=bd, hd=hd, dc=dc, pidx=pidx: e.scalar_tensor_tensor(
                            out=G32[:, hd, dc, :], in0=G32[:, hd, dc, :], scalar=ebl[:, pidx:pidx + 1], in1=P.bank[bd[dc]][:, 0:257],
                            op0=ALU.mult, op1=ALU.add), r=["G%d" % hd, "ebl", "pb%d" % bd[dc]], w=["G%d" % hd])
                    S.add("act", lambda e, hd=hd, idx=idx: e.activation(
                        out=Cbf[:, hd, :, :], in_=G32[:, hd, :, :], func=AF.Copy, scale=ebl[:, idx:idx + 1]),
                        r=["G%d" % hd, "ebl"], w=["C%d" % hd])
                    smt = "sm%d" % sb
                    S.add("act", lambda e, bacc=bacc, sb=sb, idx=idx: e.activation(
                        out=sm[:, sb, 0:1], in_=P.bank[bacc][:, 384:385], func=AF.Abs, scale=ebcol[:, idx:idx + 1]),
                        r=["pbB%d" % hd, "ebcol"], w=[smt])
                    S.add("dve", lambda e, sb=sb: e.tensor_scalar(
                        out=sm[:, sb, 1:2], in0=sm[:, sb, 0:1], scalar1=1.0, scalar2=None, op0=ALU.max), r=[smt], w=[smt])
                    S.add("dve", lambda e, sb=sb: e.reciprocal(out=sm[:, sb, 2:3], in_=sm[:, sb, 1:2]), r=[smt], w=[smt])
                    S.add("dve", lambda e, sb=sb, idx=idx: e.tensor_tensor(
                        out=sm[:, sb, 3:4], in0=sm[:, sb, 2:3], in1=ebcol[:, idx:idx + 1], op=ALU.mult), r=[smt, "ebcol"], w=[smt])
                    S.add("act", lambda e, bacc=bacc, sb=sb: e.activation(
                        out=h32[:, sb, :], in_=P.bank[bacc][:, 128:384], func=AF.Copy, scale=sm[:, sb, 3:4]),
                        r=["pbB%d" % hd, smt], w=["h32%d" % sb])
                    S.add("dve", lambda e, sb=sb: e.bn_stats(out=bst[:, sb, :], in_=h32[:, sb, :]), r=["h32%d" % sb], w=["bst%d" % sb])
                    S.add("dve", lambda e, sb=sb: e.bn_aggr(out=sm[:, sb, 4:6], in_=bst[:, sb, :]), r=["bst%d" % sb, smt], w=[smt])
                    S.add("act", lambda e, sb=sb: e.activation(out=sm[:, sb, 6:7], in_=sm[:, sb, 5:6], func=AF.Sqrt, bias=float(EPS)),
                          r=[smt], w=[smt])
                    S.add("dve", lambda e, sb=sb: e.reciprocal(out=sm[:, sb, 7:8], in_=sm[:, sb, 6:7]), r=[smt], w=[smt])
                    S.add("dve", lambda e, sb=sb: e.tensor_scalar(
                        out=h32[:, sb, :], in0=h32[:, sb, :], scalar1=sm[:, sb, 4:5], scalar2=sm[:, sb, 7:8],
                        op0=ALU.subtract, op1=ALU.mult), r=["h32%d" % sb, smt], w=["h32%d" % sb])
                    S.add("pool", lambda e, sb=sb, hd=hd: e.tensor_tensor(
                        out=h32[:, sb, :], in0=h32[:, sb, :], in1=mhg[:, hd * 256:(hd + 1) * 256], op=ALU.mult),
                        r=["h32%d" % sb, "mhg"], w=["h32%d" % sb])
                    S.add("pool", lambda e, sb=sb, hd=hd, ob=ob: e.tensor_tensor(
                        out=ytm[:, sb, :], in0=h32[:, sb, :], in1=omb[:, ob, hd * 256:(hd + 1) * 256], op=ALU.mult),
                        r=["h32%d" % sb, "omb%d" % ob], w=["ytm%d" % sb])
                    pb_ = idx % 2
                    for ec in range(2):
                        S.add("pe", lambda e, pb_=pb_, sb=sb, ec=ec: e.transpose(
                            out=pbf[pb_][:, ec * 128:(ec + 1) * 128], in_=ytm[:, sb, ec * 128:(ec + 1) * 128], identity=identb[:, :]),
                            r=["ytm%d" % sb, "identb"], w=["pbf%d" % pb_])
                    copy_op(P, P.evac_eng(), yTm[:, hd * 2:hd * 2 + 2, cs_],
                            pbf[pb_][:, 0:256].rearrange("p (a t) -> p a t", a=2), r=["pbf%d" % pb_], w=["yTm%d" % hd])
                S = S_real
                P.S = S
                if "d" in OPT:
                    for i in range(max(len(recs[0]), len(recs[1]))):
                        for rl in recs:
                            if i < len(rl):
                                S.add(*rl[i][0], **rl[i][1])
                else:
                    for rl in recs:
                        for a_, k_ in rl:
                            S.add(*a_, **k_)
            for tc in range(4):
                S.add("sp", lambda e, tc=tc: e.dma_start(
                    out=yT_out[tc, 0:512, :].rearrange("(j p) t -> p j t", p=128), in_=yTm[:, :, tc * 1024:(tc + 1) * 1024]),
                    r=["yTm0", "yTm1"], dma=True)
            S.flush()


CC_GROUPS = [[0, 1], [2, 3], [4, 5], [6, 7]]


def allgather_chunk(P, src, dst, r=(), w=()):
    P.S.add("pool", lambda e: e.collective_compute(
        "AllGather", ALU.bypass, replica_groups=CC_GROUPS, ins=[src.opt()], outs=[dst.opt()]), r=r, w=w, cc=True)


def build_fused():
    P = Prog("fused")
    nc = P.nc
    G = {}
    G["x_full"] = P.inp("x_full", [SEQ, D], F32)
    G["x_own"] = P.inp("x_own", [2048, D], F32)
    G["mem"] = P.inp("mem", [NMEM, D], F32)
    G["w_in"] = P.inp("w_in", [DEPTH, D, WSEL], F32)
    for k in ("w_out", "xq", "xk", "xv", "xo"):
        G[k] = P.inp(k, [DEPTH, D, D], F32)
    G["w1"] = P.inp("w1", [DEPTH, D, FFN], F32)
    G["w3"] = P.inp("w3", [DEPTH, D, FFN], F32)
    G["w2"] = P.inp("w2", [DEPTH, FFN, D], F32)
    G["lnp"] = P.inp("lnp", [DEPTH, 6, D], F32)
    G["lnpT"] = P.inp("lnpT", [DEPTH, 128, 96], F32)
    G["hsel"] = P.inp("hsel", [128, 2], F32)
    G["cw"] = P.inp("cw", [DEPTH, 128, 8, 4], F32)
    G["cb"] = P.inp("cb", [DEPTH, 128, 8], F32)
    G["gbb"] = P.inp("gbb", [DEPTH, 128, 128], F32)
    G["mhg"] = P.inp("mhg", [DEPTH, 128, 512], F32)
    G["biasT"] = P.inp("biasT", [4, 128, 3, 256], F32)
    G["maskc"] = P.inp("maskc", [128, 3, 256], F32)
    G["identb"] = P.inp("identb", [128, 128], BF16)
    G["identf"] = P.inp("identf", [128, 128], F32)
    G["onesb"] = P.inp("onesb", [128, 128], BF16)
    G["onesf"] = P.inp("onesf", [128, 128], F32)
    G["u01"] = P.inp("u01", [128, 128], F32)
    out = P.out("out", [2048, D], F32)
    xmid = nc.dram_tensor("xmid", [2048, D], F32, kind="Internal").ap()
    xsend = nc.dram_tensor("xsend", [4, D, TT], BF16, kind="Internal").ap()
    xrecv = nc.dram_tensor("xrecv", [4, 2, D, TT], BF16, kind="Internal").ap()
    x_in = G["x_own"]
    wcache = nc.dram_tensor("wcache", [46, 128, 16 * 512], BF16, kind="Internal").ap()
    for l in range(DEPTH):
        o = p_dram(P, l)
        ysend = nc.dram_tensor("ysend%d" % l, [4, 1024, 1024], BF16, kind="Internal").ap()
        yrecv = nc.dram_tensor("yrecv%d" % l, [4, 2, 1024, 1024], BF16, kind="Internal").ap()
        if l == 0:
            p_phase(P, l, G, o, x_tm=G["x_full"])
        else:
            p_phase(P, l, G, o, xrecv=xrecv)
        P.phase += 1
        m_phase(P, l, o, G, ysend)
        for tc in (0, 2, 1, 3):
            allgather_chunk(P, ysend[tc], yrecv[tc], w=["yrecv%d" % tc])
        last = (l == DEPTH - 1)
        r_phase(P, l, G, x_in, yrecv, out if last else xmid, None if last else xsend, xrecv, wcache)
        if not last:
            x_in = xmid
    P.S.flush(final=True)
    P.es0.close()
    return P


_PROGS = {}


def _prog(name):
    if name not in _PROGS:
        _PROGS[name] = build_fused()
    return _PROGS[name]


def _t5_bucket_np(dist):
    dist = np.asarray(dist, np.int64)
    d = np.maximum(dist, 1).astype(np.float32)
    lb = 16 + (np.log(d / np.float32(16)) / np.float32(math.log(2048 / 16)) * np.float32(16)).astype(np.int32)
    return np.where(dist < 16, dist, np.minimum(lb, 31))


def _consts():
    k = np.arange(128)[:, None]
    q = np.arange(128)[None, :]
    mask = np.zeros((128, 3, 256), np.float32)
    mask[:, :, 0:128] = np.where(k >= q, 0.0, -30000.0)[:, None, :]
    mask[:, :, 128:256] = np.where(k <= q, 0.0, -30000.0)[:, None, :]
    idx = np.zeros((3, 128, 256), np.int64)
    for pi, (dil, _) in enumerate(PATS):
        dprev = np.clip(q + 128 - k, 0, 128)
        dcur = np.clip(q - k, 0, 128)
        idx[pi, :, 0:128] = _t5_bucket_np(dprev * dil)
        idx[pi, :, 128:256] = _t5_bucket_np(dcur * dil)
    return {
        "identf": np.eye(128, dtype=np.float32),
        "identb": np.eye(128, dtype=np.float32).astype(NPBF),
        "onesb": np.ones((128, 128), np.float32).astype(NPBF),
        "onesf": np.ones((128, 128), np.float32),
        "u01": (k <= q).astype(np.float32),
        "maskc": mask,
        "bidx": idx,
    }


def _core_inputs(c, C, x, mem, w_in, gate_b, conv_w, conv_b, mh_g, rel_bias, shared):
    b, g = c // 2, c % 2
    fs = np.arange(g * 512, (g + 1) * 512)
    gcols = [2 * g, 2 * g + 1, 4 + 2 * g, 4 + 2 * g + 1]
    wcols = np.concatenate([fs, 1024 + fs, 2048 + fs, 3072 + fs, 4096 + np.array(gcols), 4104 + fs, 5128 + fs, 6152 + fs])
    chans = np.concatenate([fs, 1024 + fs])
    m = dict(shared)
    m["x_full"] = np.ascontiguousarray(x[b])
    m["x_own"] = np.ascontiguousarray(x[b, g * 2048:(g + 1) * 2048])
    m["mem"] = np.ascontiguousarray(mem[b])
    m["w_in"] = np.ascontiguousarray(w_in[:, :, wcols])
    hs = np.zeros((128, 2), np.float32)
    hs[:, g] = 1.0
    m["hsel"] = hs
    m["cw"] = np.ascontiguousarray(np.stack([conv_w[l][:, chans].T.reshape(8, 128, 4).transpose(1, 0, 2) for l in range(DEPTH)]))
    m["cb"] = np.ascontiguousarray(np.stack([conv_b[l][chans].reshape(8, 128).T for l in range(DEPTH)]))
    m["gbb"] = np.ascontiguousarray(np.stack([np.tile(gate_b[l][gcols][None, :], (128, 32)) for l in range(DEPTH)]))
    m["mhg"] = np.ascontiguousarray(np.stack([np.broadcast_to(mh_g[l][fs][None, :], (128, 512)) for l in range(DEPTH)]))
    m["biasT"] = np.ascontiguousarray(rel_bias[C["bidx"]][:, :, :, 4 * g:4 * g + 4].transpose(3, 1, 0, 2)).astype(np.float32)
    return m


def kernel(x, mem, w_in, gate_b, conv_w, conv_b, mh_g, rel_bias, w_out, xq, xk, xv, xo, w1, w3, w2, ln_g, ln_b):
    f = lambda a: np.ascontiguousarray(np.asarray(a, dtype=np.float32))
    x, mem, w_in, gate_b, conv_w, conv_b, mh_g, rel_bias = map(f, (x, mem, w_in, gate_b, conv_w, conv_b, mh_g, rel_bias))
    w_out, xq, xk, xv, xo, w1, w3, w2, ln_g, ln_b = map(f, (w_out, xq, xk, xv, xo, w1, w3, w2, ln_g, ln_b))
    C = _consts()
    perm = np.concatenate([np.arange(0, 512), np.arange(1024, 1536), np.arange(512, 1024), np.arange(1536, 2048)])
    lnp = np.ascontiguousarray(np.stack([np.stack([ln_g[l, 0], ln_b[l, 0], ln_g[l, 1], ln_b[l, 1], ln_g[l, 2], ln_b[l, 2]], 0)
                                         for l in range(DEPTH)]))
    lnpT = np.ascontiguousarray(lnp.reshape(DEPTH, 6, 16, 128).transpose(0, 3, 1, 2).reshape(DEPTH, 128, 96))
    shared = {"lnpT": lnpT, "w_out": np.ascontiguousarray(w_out[:, perm, :]), "xq": xq, "xk": xk, "xv": xv, "xo": xo, "w1": w1, "w3": w3, "w2": w2,
              "lnp": lnp, "maskc": C["maskc"], "identb": C["identb"], "identf": C["identf"], "onesb": C["onesb"],
              "onesf": C["onesf"], "u01": C["u01"]}
    maps = [_core_inputs(c, C, x, mem, w_in, gate_b, conv_w, conv_b, mh_g, rel_bias, shared) for c in range(8)]
    P = _prog("fused")
    res = run_bass_kernel_spmd(P.nc, maps, core_ids=list(range(8))).results
    out = np.zeros((NB, SEQ, D), np.float32)
    for c in range(8):
        out[c // 2, (c % 2) * 2048:(c % 2 + 1) * 2048] = np.asarray(res[c]["out"])
    return out
```
